# Optimizing a Trainium2 kernel written in Bass

```python
import math
import jax
import jax.numpy as jnp
from jax import lax
import numpy as np

D_MODEL = 2048
BATCH = 2
SEQ = 16384
DEPTH = 4

PLE_DIM = 256
HEAD_DIM = 64
A_WIDTH = D_MODEL // 2
A_HEADS = A_WIDTH // HEAD_DIM
DECAY_LORA = max(32, int(round(1.8 * A_WIDTH ** 0.5 / 32)) * 32)
AAA_LORA = max(32, int(round(1.8 * A_WIDTH ** 0.5 / 32)) * 32)
GATE_LORA = max(32, int(round(0.6 * A_WIDTH ** 0.8 / 32)) * 32)
A_COLS = 3 * A_WIDTH + DECAY_LORA + AAA_LORA + GATE_LORA
WKV_GN_EPS = 64e-5
B_WIDTH = D_MODEL // 2
B_Q_HEADS = B_WIDTH // HEAD_DIM
B_KV_HEADS = max(1, B_Q_HEADS // 8)
B_GROUP = B_Q_HEADS // B_KV_HEADS
B_KV_WIDTH = B_KV_HEADS * HEAD_DIM
B_COLS = B_WIDTH + 2 * B_KV_WIDTH
WINDOW = 128
BLOCK = 128
EVEN_COLS = A_COLS + B_COLS
EVEN_MIX = A_WIDTH + B_WIDTH
DIFF_HEAD_DIM = 128
DIFF_HEADS = D_MODEL // (2 * DIFF_HEAD_DIM)
DIFF_WIDTH = DIFF_HEADS * 2 * DIFF_HEAD_DIM
ODD_COLS = 3 * DIFF_WIDTH
ROPE_THETA = 500000.0
ROPE_FRACTION = 4
FFN_DIM = -(-8 * D_MODEL // (3 * 256)) * 256
LN_EPS = 1e-5

kernel_name = 'hybrid_rwkv7_swa_sink_diffattn_deepnorm'


def layer_norm(x, g, b):
    xf = x.astype(jnp.float32)
    mu = jnp.mean(xf, -1, keepdims=True)
    var = jnp.mean(jnp.square(xf - mu), -1, keepdims=True)
    y = (xf - mu) * lax.rsqrt(var + LN_EPS) * g.astype(jnp.float32) + b.astype(jnp.float32)
    return y.astype(x.dtype)


def partial_rotary(t, positions):
    hd = t.shape[-1]
    rd = hd // ROPE_FRACTION
    half = rd // 2
    inv_freq = ROPE_THETA ** (-jnp.arange(half, dtype=jnp.float32) / half)
    ang = positions.astype(jnp.float32)[:, :, None] * inv_freq
    cos = jnp.cos(ang)[:, :, None, :]
    sin = jnp.sin(ang)[:, :, None, :]
    tr = t[..., :rd].astype(jnp.float32)
    t1, t2 = tr[..., :half], tr[..., half:]
    rot = jnp.concatenate([t1 * cos - t2 * sin, t2 * cos + t1 * sin], -1).astype(t.dtype)
    return jnp.concatenate([rot, t[..., rd:]], -1)


def token_shift(h):
    return jnp.pad(h, ((0, 0), (1, 0), (0, 0)))[:, :-1]


def wkv7_step(state, inp):
    r_t, w_t, k_t, v_t, a_t, b_t = inp
    sa = jnp.einsum('bhij,bhj->bhi', state, a_t)
    state = (state * w_t[:, :, None, :] + sa[..., None] * b_t[:, :, None, :]
             + v_t[..., None] * k_t[:, :, None, :])
    return state, jnp.einsum('bhij,bhj->bhi', state, r_t)


def rwkv7_mix(h, mu, w0, w_up, a0, a_up, g_up, k_k, k_a, r_k, gn_w, gn_b):
    bsz, seq = h.shape[0], h.shape[1]
    f32 = jnp.float32
    h = h + (token_shift(h) - h) * mu
    r, k, v, lw, la, lg = jnp.split(
        h, [A_WIDTH, 2 * A_WIDTH, 3 * A_WIDTH, 3 * A_WIDTH + DECAY_LORA,
            3 * A_WIDTH + DECAY_LORA + AAA_LORA], axis=-1)
    w = -jax.nn.softplus(-(w0 + jnp.tanh(lw) @ w_up)) - 0.5
    decay = jnp.exp(-jnp.exp(w.astype(f32)))
    a = jax.nn.sigmoid(a0 + la @ a_up)
    g = jax.nn.sigmoid(lg) @ g_up
    heads = lambda t: t.astype(f32).reshape(bsz, seq, A_HEADS, HEAD_DIM)
    kk = heads(k * k_k)
    kk = kk / jnp.maximum(jnp.sqrt(jnp.sum(kk * kk, -1, keepdims=True)), 1e-12)
    k = k * (1.0 + (a - 1.0) * k_a)
    r_h, k_h, v_h, a_h, w_h = heads(r), heads(k), heads(v), heads(a), heads(decay)
    xs = tuple(jnp.moveaxis(t, 1, 0) for t in (r_h, w_h, k_h, v_h, -kk, kk * a_h))
    state0 = jnp.zeros((bsz, A_HEADS, HEAD_DIM, HEAD_DIM), f32)
    _, y = lax.scan(wkv7_step, state0, xs)
    y = jnp.moveaxis(y, 0, 1)
    ym = jnp.mean(y, -1, keepdims=True)
    yv = jnp.mean(jnp.square(y - ym), -1, keepdims=True)
    y = ((y - ym) * lax.rsqrt(yv + WKV_GN_EPS)).reshape(bsz, seq, A_WIDTH)
    y = y * gn_w.astype(f32) + gn_b.astype(f32)
    bonus = jnp.sum(r_h * k_h * r_k.astype(f32), -1, keepdims=True) * v_h
    y = y + bonus.reshape(bsz, seq, A_WIDTH)
    return (y * g.astype(f32)).astype(h.dtype)


def swa_sink_attention(q, k, v, sinks):
    bsz, seq = q.shape[0], q.shape[1]
    nb = seq // BLOCK
    qb = q.reshape(bsz, nb, BLOCK, B_KV_HEADS, B_GROUP, HEAD_DIM)
    kb = k.reshape(bsz, nb, BLOCK, B_KV_HEADS, HEAD_DIM)
    vb = v.reshape(bsz, nb, BLOCK, B_KV_HEADS, HEAD_DIM)
    with_prev = lambda t: jnp.concatenate(
        [jnp.concatenate([jnp.zeros_like(t[:, :1]), t[:, :-1]], axis=1), t], axis=2)
    kw, vw = with_prev(kb), with_prev(vb)
    s = jnp.einsum('bnqhgd,bnkhd->bnhgqk', qb, kw).astype(jnp.float32) * HEAD_DIM ** -0.5
    rel = (jnp.arange(BLOCK)[:, None] + BLOCK) - jnp.arange(2 * BLOCK)[None, :]
    band = (rel >= 0) & (rel < WINDOW)
    key_ok = (jnp.arange(nb)[:, None] * BLOCK - BLOCK + jnp.arange(2 * BLOCK)[None, :]) >= 0
    mask = band[None] & key_ok[:, None, :]
    s = jnp.where(mask[None, :, None, None], s, -jnp.inf)
    sink = sinks.astype(jnp.float32).reshape(1, 1, B_KV_HEADS, B_GROUP, 1, 1)
    m = jnp.maximum(jnp.max(s, -1, keepdims=True), sink)
    e = jnp.exp(s - m)
    prob = e / (jnp.sum(e, -1, keepdims=True) + jnp.exp(sink - m))
    out = jnp.einsum('bnhgqk,bnkhd->bnqhgd', prob.astype(v.dtype), vw)
    return out.reshape(bsz, seq, B_WIDTH)


def diff_attention(q, k, v, lam, lam_init, subln_g):
    bsz, seq = q.shape[0], q.shape[1]
    nb = seq // BLOCK
    scale = DIFF_HEAD_DIM ** -0.5
    q_blocks = jnp.moveaxis(q.reshape(bsz, nb, BLOCK, DIFF_HEADS, 2, DIFF_HEAD_DIM), 1, 0)
    key_pos = jnp.arange(seq)

    def one_block(args):
        q_blk, n = args
        s = jnp.einsum('bqhcd,bkhcd->bchqk', q_blk, k).astype(jnp.float32) * scale
        q_pos = n * BLOCK + jnp.arange(BLOCK)
        s = jnp.where(key_pos[None, :] <= q_pos[:, None], s, -jnp.inf)
        prob = jax.nn.softmax(s, axis=-1)
        attn = prob[:, 0] - lam * prob[:, 1]
        return jnp.einsum('bhqk,bkhe->bqhe', attn.astype(v.dtype), v)

    out = lax.map(one_block, (q_blocks, jnp.arange(nb)))
    out = jnp.moveaxis(out, 0, 1).reshape(bsz, seq, DIFF_HEADS, 2 * DIFF_HEAD_DIM).astype(jnp.float32)
    out = out * lax.rsqrt(jnp.mean(jnp.square(out), -1, keepdims=True) + 1e-5)
    out = out * subln_g.astype(jnp.float32) * (1.0 - lam_init)
    return out.reshape(bsz, seq, DIFF_WIDTH).astype(v.dtype)


def swiglu(x, w1, w3, w2):
    return (jax.nn.silu(x @ w1) * (x @ w3)) @ w2


def setup_inputs(seed: int = 0) -> dict:
    key = jax.random.key(seed)
    ks = iter(jax.random.split(key, 48))
    f32 = jnp.float32
    n_even = (DEPTH + 1) // 2
    n_odd = DEPTH // 2
    beta = (8.0 * DEPTH) ** -0.25
    nrm = lambda shape, scale: jax.random.normal(next(ks), shape, f32) * scale
    unif = lambda shape, lo, hi: jax.random.uniform(next(ks), shape, f32, lo, hi)
    x = nrm((BATCH, SEQ, D_MODEL), 1.0)
    p = nrm((DEPTH, BATCH, SEQ, PLE_DIM), 1.0)
    positions = (jnp.arange(SEQ, dtype=jnp.int32)[None, :]
                 + jax.random.randint(next(ks), (BATCH, 1), 0, 1024, jnp.int32))
    return {
        'x': x,
        'p': p,
        'positions': positions,
        'e_w_in': nrm((n_even, D_MODEL, EVEN_COLS), D_MODEL ** -0.5),
        'e_mu': unif((n_even, A_COLS), 0.0, 1.0),
        'e_w0': unif((n_even, A_WIDTH), -6.0, 1.0),
        'e_w_up': nrm((n_even, DECAY_LORA, A_WIDTH), 0.1),
        'e_a0': nrm((n_even, A_WIDTH), 0.1),
        'e_a_up': nrm((n_even, AAA_LORA, A_WIDTH), AAA_LORA ** -0.5),
        'e_g_up': nrm((n_even, GATE_LORA, A_WIDTH), GATE_LORA ** -0.5),
        'e_k_k': 0.85 + nrm((n_even, A_WIDTH), 0.05),
        'e_k_a': 1.0 + nrm((n_even, A_WIDTH), 0.05),
        'e_r_k': nrm((n_even, A_HEADS, HEAD_DIM), 0.1),
        'e_gn_w': 1.0 + nrm((n_even, A_WIDTH), 0.05),
        'e_gn_b': nrm((n_even, A_WIDTH), 0.02),
        'e_sinks': nrm((n_even, B_Q_HEADS), 1.0),
        'e_w_out': nrm((n_even, EVEN_MIX, D_MODEL), EVEN_MIX ** -0.5 * beta),
        'o_w_in': nrm((n_odd, D_MODEL, ODD_COLS), D_MODEL ** -0.5),
        'o_lambda': nrm((n_odd, 4, DIFF_HEAD_DIM), 0.1),
        'o_subln_g': 1.0 + nrm((n_odd, 2 * DIFF_HEAD_DIM), 0.05),
        'o_w_out': nrm((n_odd, DIFF_WIDTH, D_MODEL), DIFF_WIDTH ** -0.5 * beta),
        'ln1_g': 1.0 + nrm((DEPTH, D_MODEL), 0.05),
        'ln1_b': nrm((DEPTH, D_MODEL), 0.02),
        'ln2_g': 1.0 + nrm((DEPTH, D_MODEL), 0.05),
        'ln2_b': nrm((DEPTH, D_MODEL), 0.02),
        'ffn_w1': nrm((DEPTH, D_MODEL, FFN_DIM), D_MODEL ** -0.5),
        'ffn_w3': nrm((DEPTH, D_MODEL, FFN_DIM), D_MODEL ** -0.5),
        'ffn_w2': nrm((DEPTH, FFN_DIM, D_MODEL), FFN_DIM ** -0.5 * beta),
        'ple_w_proj': nrm((DEPTH, PLE_DIM, D_MODEL), PLE_DIM ** -0.5),
        'ple_w_gate': nrm((DEPTH, D_MODEL, D_MODEL), D_MODEL ** -0.5),
    }


def reference(x, p, positions, e_w_in, e_mu, e_w0, e_w_up, e_a0, e_a_up, e_g_up,
              e_k_k, e_k_a, e_r_k, e_gn_w, e_gn_b, e_sinks, e_w_out,
              o_w_in, o_lambda, o_subln_g, o_w_out,
              ln1_g, ln1_b, ln2_g, ln2_b, ffn_w1, ffn_w3, ffn_w2,
              ple_w_proj, ple_w_gate):
    bsz, seq = x.shape[0], x.shape[1]
    alpha = (2.0 * DEPTH) ** 0.25
    for i in range(DEPTH):
        j = i // 2
        if i % 2 == 0:
            h = x @ e_w_in[j]
            ha, hq, hk, hv = jnp.split(
                h, [A_COLS, A_COLS + B_WIDTH, A_COLS + B_WIDTH + B_KV_WIDTH], axis=-1)
            ya = rwkv7_mix(ha, e_mu[j], e_w0[j], e_w_up[j], e_a0[j], e_a_up[j], e_g_up[j],
                           e_k_k[j], e_k_a[j], e_r_k[j], e_gn_w[j], e_gn_b[j])
            q = partial_rotary(hq.reshape(bsz, seq, B_Q_HEADS, HEAD_DIM), positions)
            k = partial_rotary(hk.reshape(bsz, seq, B_KV_HEADS, HEAD_DIM), positions)
            v = hv.reshape(bsz, seq, B_KV_HEADS, HEAD_DIM)
            yb = swa_sink_attention(q, k, v, e_sinks[j])
            mix = jnp.concatenate([ya, yb], axis=-1) @ e_w_out[j]
        else:
            h = x @ o_w_in[j]
            hq, hk, hv = jnp.split(h, [DIFF_WIDTH, 2 * DIFF_WIDTH], axis=-1)
            q = partial_rotary(hq.reshape(bsz, seq, 2 * DIFF_HEADS, DIFF_HEAD_DIM), positions)
            k = partial_rotary(hk.reshape(bsz, seq, 2 * DIFF_HEADS, DIFF_HEAD_DIM), positions)
            q = q.reshape(bsz, seq, DIFF_HEADS, 2, DIFF_HEAD_DIM)
            k = k.reshape(bsz, seq, DIFF_HEADS, 2, DIFF_HEAD_DIM)
            v = hv.reshape(bsz, seq, DIFF_HEADS, 2 * DIFF_HEAD_DIM)
            lam_init = 0.8 - 0.6 * math.exp(-0.3 * i)
            lv = o_lambda[j].astype(jnp.float32)
            lam = jnp.exp(jnp.sum(lv[0] * lv[1])) - jnp.exp(jnp.sum(lv[2] * lv[3])) + lam_init
            mix = diff_attention(q, k, v, lam, lam_init, o_subln_g[j]) @ o_w_out[j]
        x = layer_norm(alpha * x + mix, ln1_g[i], ln1_b[i])
        x = layer_norm(alpha * x + swiglu(x, ffn_w1[i], ffn_w3[i], ffn_w2[i]), ln2_g[i], ln2_b[i])
        x = x + jax.nn.sigmoid(x @ ple_w_gate[i]) * (p[i] @ ple_w_proj[i])
    return x
```

```python
import math
from contextlib import ExitStack
import numpy as np
import concourse.bass as bass
import concourse.mybir as mybir
from concourse.bass import ds
from concourse.bass_utils import run_bass_kernel_spmd

F32 = mybir.dt.float32
BF16 = mybir.dt.bfloat16
I32 = mybir.dt.int32
AF = mybir.ActivationFunctionType
ALU = mybir.AluOpType
AX = mybir.AxisListType

D = 2048
FF = 5632
NFC = FF // 128
DEPTH = 4
ALPHA = (2.0 * DEPTH) ** 0.25
LN_EPS = 1e-5
RG = [[0, 1, 2, 3], [4, 5, 6, 7]]


class Eng:
    def __init__(self, h, sems):
        self.h = h
        self.sems = sems
        self.si = 0
        self.n = 0
        self.waited = {}


class Ctx:
    def __init__(self, nc, stack):
        self.nc = nc
        self.stack = stack
        self.semlist = []
        self.E = {}
        for name, h in (("pe", nc.tensor), ("act", nc.scalar), ("dve", nc.vector), ("pool", nc.gpsimd), ("sp", nc.sync)):
            sems = [self.newsem(f"s_{name}{i}") for i in range(6 if name != "sp" else 1)]
            self.E[name] = Eng(h, sems)
        self.lastw = {}
        self.readers = {}
        self.dsem = {}
        self.dcount = {}
        self.cc_sem = self.newsem("s_cc")
        self.cc_n = 0

    def newsem(self, name):
        s = self.stack.enter_context(self.nc.semaphore(name))
        self.semlist.append(s)
        return len(self.semlist) - 1

    def need(self, e, tok):
        if tok is None:
            return
        sid, val = tok
        if sid in self.dcount:
            val = max(val, self.dcount[sid])
        if e.waited.get(sid, 0) >= val:
            return
        e.h.wait_ge(self.semlist[sid], val)
        e.waited[sid] = val

    def _deps(self, e, reads, writes):
        for k in reads:
            self.need(e, self.lastw.get(k))
        for k in writes:
            self.need(e, self.lastw.get(k))
            for sid, val in list(self.readers.get(k, {}).items()):
                self.need(e, (sid, val))

    def _record(self, tok, reads, writes):
        for k in writes:
            self.lastw[k] = tok
            self.readers[k] = {}
        for k in reads:
            if k in writes:
                continue
            d = self.readers.setdefault(k, {})
            d[tok[0]] = max(d.get(tok[0], 0), tok[1])

    def op(self, eng, reads, writes, fn):
        e = self.E[eng]
        self._deps(e, reads, writes)
        ins = fn(e.h)
        if e.n >= 100000:
            e.si += 1
            e.n = 0
        e.n += 1
        sid = e.sems[e.si]
        ins.then_inc(self.semlist[sid], 1)
        tok = (sid, e.n)
        e.waited[sid] = max(e.waited.get(sid, 0), 0)
        self._record(tok, reads, writes)
        return tok

    def dma(self, q, skey, out, in_, reads, writes, **kw):
        e = self.E[q]
        self._deps(e, reads, writes)
        if skey not in self.dsem:
            sid = self.newsem("d_" + str(skey))
            self.dsem[skey] = sid
            self.dcount[sid] = 0
        sid = self.dsem[skey]
        ins = e.h.dma_start(out=out, in_=in_, **kw)
        self.dcount[sid] += 16
        ins.then_inc(self.semlist[sid], 16)
        tok = (sid, self.dcount[sid])
        self._record(tok, reads, writes)
        return tok

    def collective(self, kind, ins, outs, reads, writes):
        e = self.E["pool"]
        self._deps(e, reads, writes)
        i = e.h.collective_compute(kind, ALU.bypass, replica_groups=RG, ins=ins, outs=outs)
        self.cc_n += 1
        i.then_inc(self.semlist[self.cc_sem])
        tok = (self.cc_sem, self.cc_n)
        self._record(tok, reads, writes)
        return tok

    def barrier(self):
        toks = []
        for name, e in self.E.items():
            if e.n > 0:
                toks.append((e.sems[e.si], e.n))
        for sid, c in self.dcount.items():
            if c > 0:
                toks.append((sid, c))
        if self.cc_n:
            toks.append((self.cc_sem, self.cc_n))
        for name, e in self.E.items():
            for t in toks:
                self.need(e, t)
        self.lastw = {}
        self.readers = {}


def build(S, layers=(0, 1, 2, 3), dbg=None, mode="full"):
    TPC = S // 4
    TB = min(256, TPC)
    NSB = TB // 128
    NT = TPC // TB
    nc = bass.Bass("TRN2", target_bir_lowering=False)
    dbg = dbg or {}

    def din(name, shape, dt=F32):
        return nc.dram_tensor(name, list(shape), dt, kind="ExternalInput").ap()

    def dint(name, shape, dt):
        return nc.dram_tensor(name, list(shape), dt).ap()

    NLY = len(layers)
    NE = max(1, len([l for l in layers if l % 2 == 0]))
    NO = max(1, len([l for l in layers if l % 2 == 1]))
    LI = {l: i for i, l in enumerate(layers)}
    JI = {}
    for l in layers:
        JI[l] = len([m for m in layers if m % 2 == l % 2 and m < l])
    I = {}
    I["x_own"] = din("x_own", [TPC, D])
    I["p_own"] = din("p_own", [NLY, TPC, 256])
    I["pos"] = din("pos", [1, S], I32)
    I["ewin"] = din("ewin", [NE, D, 1536])
    I["ewv"] = din("ewv", [NE, D, 64])
    I["emu"] = din("emu", [NE, 128, 9])
    I["eprm"] = din("eprm", [NE, 128, 14])
    I["ewaup"] = din("ewaup", [NE, 128, 256])
    I["egup"] = din("egup", [NE, 2, 128, 256])
    I["esink"] = din("esink", [NE, 1, 4])
    I["ewout"] = din("ewout", [NE, D, D])
    I["owin"] = din("owin", [NO, D, 1536])
    I["olam"] = din("olam", [NO, 4, 128])
    I["osub"] = din("osub", [NO, 1, 256])
    I["owout"] = din("owout", [NO, D, D])
    for nm in ("ln1g", "ln1b", "ln2g", "ln2b"):
        I[nm] = din(nm, [NLY, 1, D])
    I["w1"] = din("w1", [NLY, D, FF])
    I["w3"] = din("w3", [NLY, D, FF])
    I["w2"] = din("w2", [NLY, FF, D])
    I["wpp"] = din("wpp", [NLY, 256, D])
    I["wpg"] = din("wpg", [NLY, D, D])
    I["cst"] = din("cst", [128, 8, 128])
    I["cstv"] = din("cstv", [128, 8])
    I["cmask"] = din("cmask", [128, 6, 512])
    if mode == "b":
        I["yT_in"] = din("yT_in", [4 * NT * 512, TB])
    out_d = nc.dram_tensor("out", [TPC, D], F32, kind="ExternalOutput").ap()
    DBG = {}
    for k, shp in dbg.items():
        DBG[k] = nc.dram_tensor("dbg_" + k, list(shp), F32, kind="ExternalOutput").ap()

    Wb = {}
    Wb["wout"] = dint("wout_bf", [NLY, D, D], BF16)
    Wb["w1"] = dint("w1_bf", [NLY, NFC, D, 128], BF16)
    Wb["w3"] = dint("w3_bf", [NLY, NFC, D, 128], BF16)
    Wb["w2"] = dint("w2_bf", [NLY, FF, D], BF16)
    Wb["wg"] = dint("wg_bf", [NLY, D, D], BF16)
    Wb["wp"] = dint("wp_bf", [NLY, 256, D], BF16)
    Wb["ewin"] = dint("ewin_bf", [NE, D, 1536], BF16)
    Wb["ewv"] = dint("ewv_bf", [NE, D, 64], BF16)
    Wb["owin"] = dint("owin_bf", [NO, D, 1536], BF16)
    x_res = dint("x_res", [TPC, D], F32)
    xT_own = dint("xT_own", [D, TPC], BF16)
    MB1 = 1 << 20
    CH_X = max(1, (D * TPC * 2) // MB1)
    RX = D // CH_X
    xT_all = dint("xT_all", [4 * D, TPC], BF16)

    def load_xT(cx, key, dst, rp, t0, TT, dkey):
        ncp = RX // 128
        for k in range(CH_X):
            base = (k * 4 + rp) * RX
            cx.dma("sp", key, dst[:, k * ncp:(k + 1) * ncp, :], xT_all[base:base + RX, t0:t0 + TT].rearrange("(c p) t -> p c t", p=128),
                   ["xT_all"], [dkey])
    yT_own = dint("yT_own", [4 * NT * 512, TB], BF16)
    KPJ = max(1, (NT * 512 * TB * 2) // MB1)
    RY = (NT * 512) // KPJ
    CH_Y = 4 * KPJ
    yT_all = dint("yT_all", [16 * NT * 512, TB], BF16)

    y_mine = dint("y_mine", [4 * NT * 512, TB], BF16)

    def ydst(jr, tt, c0, ncols):
        r0 = (jr * NT + tt // TB) * 512 + c0
        return yT_own[r0:r0 + ncols, (tt % TB):(tt % TB) + 128]

    with ExitStack() as stack:
        cx = Ctx(nc, stack)
        block = stack.enter_context(nc.Block())

        @block.sync
        def _(sync):
            rank = sync.partition_id() % 4
            rank_act = nc.scalar.partition_id() % 4
            gy_count = [0]
            ps = stack.enter_context(nc.psum_tensor("ps", [128, 6, 512], F32))
            psb = stack.enter_context(nc.psum_tensor("psb", [128, 2, 1024], BF16))

            def conv(skey, dst2d, src2d, rows, maxd=4096):
                for i in range(0, rows, 128):
                    cx.dma("pool", skey, dst2d[i:i + 128], src2d[i:i + 128], [], [skey], max_dma_last_dim=maxd)

            def convert_weights():
                for L0 in layers:
                    j = JI[L0]
                    L = LI[L0]
                    conv("c_wout", Wb["wout"][L], (I["ewout"] if L0 % 2 == 0 else I["owout"])[j], D)
                    conv("c_wg", Wb["wg"][L], I["wpg"][L], D)
                    conv("c_wp", Wb["wp"][L], I["wpp"][L], 256)
                    conv("c_w2", Wb["w2"][L], I["w2"][L], FF)
                    for nm in ("w1", "w3"):
                        for i in range(0, D, 128):
                            cx.dma("pool", "c_" + nm, Wb[nm][L, :, i:i + 128, :].rearrange("f d c -> d f c"),
                                   I[nm][L, i:i + 128, :].rearrange("d (f c) -> d f c", c=128), [], ["c_" + nm],
                                   max_dma_last_dim=512)
                    if L0 % 2 == 0:
                        conv("c_ewin", Wb["ewin"][j], I["ewin"][j], D)
                        conv("c_ewv", Wb["ewv"][j], I["ewv"][j], D)
                    else:
                        conv("c_owin", Wb["owin"][j], I["owin"][j], D)

            WKEYS = ["c_wout", "c_wg", "c_wp", "c_w2", "c_w1", "c_w3", "c_ewin", "c_ewv", "c_owin"]

            pst = ExitStack()
            stack.enter_context(pst)
            ident_f = pst.enter_context(nc.sbuf_tensor("ident_f", [128, 128], F32))
            ident_b = pst.enter_context(nc.sbuf_tensor("ident_b", [128, 128], BF16))
            cx.dma("sp", "cst", ident_f[:], I["cst"][:, 0, :], [], ["ident_f"])
            cx.op("dve", ["ident_f"], ["ident_b"], lambda h: h.tensor_copy(out=ident_b[:], in_=ident_f[:]))

            def emit_xT(xs_f32_ap, xkey, tcol, tmpb, tmpkey, xtb, xtkey, dst, dstkey):
                cx.op("act", [xkey], [tmpkey], lambda h: h.activation(out=tmpb, in_=xs_f32_ap, func=AF.Copy))
                for half in range(2):
                    for c in range(8):
                        cc = half * 8 + c
                        cx.op("pe", [tmpkey, "ident_b"], [("psb", half)],
                              lambda h, cc=cc, c=c, half=half: h.transpose(out=psb[:, half, c * 128:(c + 1) * 128],
                                                                            in_=tmpb[:, cc * 128:(cc + 1) * 128], identity=ident_b[:]))
                    cx.op("dve" if half == 0 else "act", [("psb", half)], [xtkey + str(half)],
                          (lambda h, half=half: h.tensor_copy(out=xtb[:, half * 8:(half + 1) * 8, :],
                                                              in_=psb[:, half, :].rearrange("p (c t) -> p c t", t=128)))
                          if half == 0 else
                          (lambda h, half=half: h.activation(out=xtb[:, half * 8:(half + 1) * 8, :],
                                                             in_=psb[:, half, :].rearrange("p (c t) -> p c t", t=128), func=AF.Copy)))
                cx.dma("sp", dstkey, dst.rearrange("(c p) t -> p c t", p=128)[:, :, tcol:tcol + 128], xtb,
                       [xtkey + "0", xtkey + "1"], [dstkey])

            def phase_b(L0, last):
                L = LI[L0]
                with ExitStack() as bs:
                    def sb(name, shape, dt):
                        return bs.enter_context(nc.sbuf_tensor("B%d_" % L0 + name, list(shape), dt))
                    wbuf = sb("wbuf", [128, 16, 2048], BF16)
                    w13 = sb("w13", [128, 2, 2, 16, 128], BF16)
                    yT = sb("yT", [128, 16, TB], BF16)
                    xz = sb("xz", [128, NSB, 2048], F32)
                    x1T = sb("x1T", [128, 16, TB], BF16)
                    hdnT = sb("hdnT", [128, NFC, TB], BF16)
                    z2 = sb("z2", [128, NSB, 2048], F32)
                    lng = sb("lng", [128, 2048], F32)
                    lnb = sb("lnb", [128, 2048], F32)
                    tmpf = sb("tmpf", [128, 2048], F32)
                    tmpb = sb("tmpb", [128, 2048], BF16)
                    xtb = sb("xtb", [128, 16, 128], BF16)
                    pT = sb("pT", [128, 2, TB], BF16)
                    pblk = sb("pblk", [128, 256], F32)
                    wpb = sb("wpb", [128, 2, 2048], BF16)
                    st6 = sb("st6", [128, 4, 6], F32)
                    mv = sb("mv", [128, 2], F32)
                    rstd = sb("rstd", [128, 1], F32)
                    silu = sb("silu", [128, TB], F32)

                    cx.dma("sp", "wpb", wpb[:], Wb["wp"][L].rearrange("(c p) n -> p c n", p=128), ["c_wp"], ["wpb"])

                    def layernorm(zap, zkey, gsrc, bsrc, outap, outkey):
                        cx.dma("sp", "lng", lng[:], gsrc.partition_broadcast(128), [], ["lng"])
                        cx.dma("sp", "lnb", lnb[:], bsrc.partition_broadcast(128), [], ["lnb"])
                        for q in range(4):
                            cx.op("dve", [zkey], ["st6"], lambda h, q=q: h.bn_stats(out=st6[:, q, :], in_=zap[:, q * 512:(q + 1) * 512]))
                        cx.op("dve", ["st6"], ["mv"], lambda h: h.bn_aggr(out=mv[:], in_=st6[:].rearrange("p a b -> p (a b)")))
                        cx.op("act", ["mv"], ["rstd"], lambda h: h.activation(out=rstd[:], in_=mv[:, 1:2], func=AF.Sqrt, bias=LN_EPS, scale=1.0))
                        cx.op("dve", ["rstd"], ["rstd"], lambda h: h.reciprocal(out=rstd[:], in_=rstd[:]))
                        cx.op("dve", [zkey, "mv", "rstd"], ["tmpf"], lambda h: h.tensor_scalar(out=tmpf[:], in0=zap, scalar1=mv[:, 0:1], scalar2=rstd[:, 0:1], op0=ALU.subtract, op1=ALU.mult))
                        cx.op("pool", ["tmpf", "lng"], ["tmpf"], lambda h: h.tensor_tensor(out=tmpf[:], in0=tmpf[:], in1=lng[:], op=ALU.mult))
                        cx.op("dve", ["tmpf", "lnb"], [outkey], lambda h: h.tensor_tensor(out=outap, in0=tmpf[:], in1=lnb[:], op=ALU.add))

                    def to_T(src_ap, srckey, dstT, dstkey, sbi):
                        cx.op("act", [srckey], ["tmpb"], lambda h: h.activation(out=tmpb[:], in_=src_ap, func=AF.Copy))
                        for half in range(2):
                            for c in range(8):
                                cc = half * 8 + c
                                cx.op("pe", ["tmpb", "ident_b"], [("psb", half)],
                                      lambda h, cc=cc, c=c, half=half: h.transpose(out=psb[:, half, c * 128:(c + 1) * 128],
                                                                                    in_=tmpb[:, cc * 128:(cc + 1) * 128], identity=ident_b[:]))
                            if half == 0:
                                cx.op("dve", [("psb", half)], [dstkey], lambda h, half=half: h.tensor_copy(
                                    out=dstT[:, half * 8:(half + 1) * 8, sbi * 128:(sbi + 1) * 128],
                                    in_=psb[:, half, :].rearrange("p (c t) -> p c t", t=128)))
                            else:
                                cx.op("act", [("psb", half)], [dstkey], lambda h, half=half: h.activation(
                                    out=dstT[:, half * 8:(half + 1) * 8, sbi * 128:(sbi + 1) * 128],
                                    in_=psb[:, half, :].rearrange("p (c t) -> p c t", t=128), func=AF.Copy))

                    for ti in range(NT):
                        t0 = ti * TB
                        for rp in range(4):
                            src = y_mine[(rp * NT + ti) * 512:(rp * NT + ti + 1) * 512, :].rearrange("(c p) t -> p c t", p=128)
                            cx.dma("sp", "yT", yT[:, rp * 4:(rp + 1) * 4, :], src, ["y_mine"], ["yT"])
                        cx.dma("sp", "wbuf", wbuf[:], Wb["wout"][L].rearrange("(c p) n -> p c n", p=128), ["c_wout"], ["wbuf"])
                        for sbi in range(NSB):
                            cx.dma("sp", "xz", xz[:, sbi, :], x_res[t0 + sbi * 128:t0 + (sbi + 1) * 128, :], ["x_res"], [("xz", sbi)])
                            for ct in range(4):
                                bank = ct % 4
                                for c in range(16):
                                    cx.op("pe", ["yT", "wbuf"], [("ps", bank)],
                                          lambda h, c=c, ct=ct, sbi=sbi, bank=bank: h.matmul(ps[:, bank, :], lhsT=yT[:, c, sbi * 128:(sbi + 1) * 128],
                                                                                          rhs=wbuf[:, c, ct * 512:(ct + 1) * 512], start=(c == 0), stop=(c == 15)))
                                cx.op("dve", [("ps", bank), ("xz", sbi)], [("xz", sbi)],
                                      lambda h, ct=ct, sbi=sbi, bank=bank: h.scalar_tensor_tensor(out=xz[:, sbi, ct * 512:(ct + 1) * 512], in0=xz[:, sbi, ct * 512:(ct + 1) * 512],
                                                                                               scalar=ALPHA, in1=ps[:, bank, :], op0=ALU.mult, op1=ALU.add))
                            layernorm(xz[:, sbi, :], ("xz", sbi), I["ln1g"][L], I["ln1b"][L], xz[:, sbi, :], ("xz", sbi))
                            to_T(xz[:, sbi, :], ("xz", sbi), x1T, "x1T", sbi)
                        if "x1" in DBG and ti == 0:
                            cx.dma("sp", "dbg", DBG["x1"][0:128, :], xz[:, 0, :], [("xz", 0)], ["dbgo"])
                        for fc in range(NFC):
                            bf = fc % 2
                            for wi, nm in enumerate(("w1", "w3")):
                                cx.dma("sp", "w13_%d" % bf, w13[:, bf, wi], Wb[nm][L, fc].rearrange("(c p) n -> p c n", p=128),
                                       ["c_" + nm], [("w13", bf, wi)])
                            for wi in range(2):
                                bank = 4 + wi
                                for c in range(16):
                                    cx.op("pe", [("w13", bf, wi), "x1T"], [("ps", bank)],
                                          lambda h, c=c, wi=wi, bf=bf, bank=bank: h.matmul(ps[:, bank, 0:TB], lhsT=w13[:, bf, wi, c, :], rhs=x1T[:, c, :],
                                                                                        start=(c == 0), stop=(c == 15)))
                            cx.op("act", [("ps", 4)], ["silu"], lambda h: h.activation(out=silu[:], in_=ps[:, 4, 0:TB], func=AF.Silu))
                            cx.op("dve", ["silu", ("ps", 5)], [("hdnT", fc)], lambda h, fc=fc: h.tensor_tensor(out=hdnT[:, fc, :], in0=silu[:], in1=ps[:, 5, 0:TB], op=ALU.mult))
                        hk = [("hdnT", fc) for fc in range(NFC)]
                        for ct in range(4):
                            w2v = wbuf[:].rearrange("p a b -> p (a b)")[:, 0:NFC * 512].rearrange("p (f n) -> p f n", n=512)
                            cx.dma("sp", "wbuf", w2v, Wb["w2"][L][:, ct * 512:(ct + 1) * 512].rearrange("(f p) n -> p f n", p=128), ["c_w2"], ["wbuf"])
                            for sbi in range(NSB):
                                bank = (ct * NSB + sbi) % 4
                                for fc in range(NFC):
                                    cx.op("pe", hk + ["wbuf"], [("ps", bank)],
                                          lambda h, fc=fc, sbi=sbi, bank=bank: h.matmul(ps[:, bank, :], lhsT=hdnT[:, fc, sbi * 128:(sbi + 1) * 128], rhs=w2v[:, fc, :],
                                                                                     start=(fc == 0), stop=(fc == NFC - 1)))
                                cx.op("dve", [("ps", bank), ("xz", sbi)], [("z2", sbi)],
                                      lambda h, ct=ct, sbi=sbi, bank=bank: h.scalar_tensor_tensor(out=z2[:, sbi, ct * 512:(ct + 1) * 512], in0=xz[:, sbi, ct * 512:(ct + 1) * 512],
                                                                                               scalar=ALPHA, in1=ps[:, bank, :], op0=ALU.mult, op1=ALU.add))
                        for sbi in range(NSB):
                            layernorm(z2[:, sbi, :], ("z2", sbi), I["ln2g"][L], I["ln2b"][L], z2[:, sbi, :], ("z2", sbi))
                            to_T(z2[:, sbi, :], ("z2", sbi), x1T, "x1T", sbi)
                            cx.dma("sp", "pblk", pblk[:], I["p_own"][L, t0 + sbi * 128:t0 + (sbi + 1) * 128, :], [], ["pblk"])
                            cx.op("act", ["pblk"], ["tmpb"], lambda h: h.activation(out=tmpb[:, 0:256], in_=pblk[:], func=AF.Copy))
                            for c in range(2):
                                cx.op("pe", ["tmpb", "ident_b"], [("psb", 0)], lambda h, c=c: h.transpose(out=psb[:, 0, c * 128:(c + 1) * 128], in_=tmpb[:, c * 128:(c + 1) * 128], identity=ident_b[:]))
                            cx.op("dve", [("psb", 0)], ["pT"], lambda h, sbi=sbi: h.tensor_copy(out=pT[:, :, sbi * 128:(sbi + 1) * 128], in_=psb[:, 0, 0:256].rearrange("p (c t) -> p c t", t=128)))
                        if "x2" in DBG and ti == 0:
                            cx.dma("sp", "dbg", DBG["x2"][0:128, :], z2[:, 0, :], [("z2", 0)], ["dbgo"])
                        cx.dma("sp", "wbuf", wbuf[:], Wb["wg"][L].rearrange("(c p) n -> p c n", p=128), ["c_wg"], ["wbuf"])
                        for sbi in range(NSB):
                            for ct in range(4):
                                bg = ct % 2
                                bp = 2 + ct % 2
                                for c in range(16):
                                    cx.op("pe", ["x1T", "wbuf"], [("ps", bg)],
                                          lambda h, c=c, ct=ct, sbi=sbi, bg=bg: h.matmul(ps[:, bg, :], lhsT=x1T[:, c, sbi * 128:(sbi + 1) * 128], rhs=wbuf[:, c, ct * 512:(ct + 1) * 512],
                                                                                      start=(c == 0), stop=(c == 15)))
                                for c in range(2):
                                    cx.op("pe", ["pT", "wpb"], [("ps", bp)],
                                          lambda h, c=c, ct=ct, sbi=sbi, bp=bp: h.matmul(ps[:, bp, :], lhsT=pT[:, c, sbi * 128:(sbi + 1) * 128], rhs=wpb[:, c, ct * 512:(ct + 1) * 512],
                                                                                      start=(c == 0), stop=(c == 1)))
                                cx.op("act", [("ps", bg)], ["tmpf"], lambda h, ct=ct, bg=bg: h.activation(out=tmpf[:, ct * 512:(ct + 1) * 512], in_=ps[:, bg, :], func=AF.Sigmoid))
                                cx.op("dve", ["tmpf", ("ps", bp)], ["tmpf"], lambda h, ct=ct, bp=bp: h.tensor_tensor(out=tmpf[:, ct * 512:(ct + 1) * 512], in0=tmpf[:, ct * 512:(ct + 1) * 512], in1=ps[:, bp, :], op=ALU.mult))
                            cx.op("pool", ["tmpf", ("z2", sbi)], [("z2", sbi)], lambda h, sbi=sbi: h.tensor_tensor(out=z2[:, sbi, :], in0=z2[:, sbi, :], in1=tmpf[:], op=ALU.add))
                            rows = slice(t0 + sbi * 128, t0 + (sbi + 1) * 128)
                            if last:
                                cx.dma("sp", "outd", out_d[rows, :], z2[:, sbi, :], [("z2", sbi)], ["outd"])
                            else:
                                cx.dma("sp", "x_res", x_res[rows, :], z2[:, sbi, :], [("z2", sbi)], ["x_res"])
                                emit_xT(z2[:, sbi, :], ("z2", sbi), t0 + sbi * 128, tmpb[:], "tmpb", xtb[:], "xtb", xT_own, "xT_own")
                cx.barrier()


            TWO_PI = 2.0 * math.pi
            C1 = 6.28125
            C2 = TWO_PI - C1

            def rotary_tables(sbx, tg0, TT, invf_ap, cosF, sinF, posi, posf, u, ki, kf):
                cx.dma("sp", "posi", posi[:, 0:TT], I["pos"][:, tg0:tg0 + TT].partition_broadcast(128), [], ["posi"])
                cx.op("dve", ["posi"], ["posf"], lambda h: h.tensor_copy(out=posf[:, 0:TT], in_=posi[:, 0:TT]))
                cx.op("dve", ["posf"], ["posf"], lambda h: h.tensor_scalar(out=posf[:, 0:TT], in0=posf[:, 0:TT], scalar1=invf_ap, scalar2=None, op0=ALU.mult))
                for (dst, dkey, off) in ((sinF, "sinF", 0.0), (cosF, "cosF", 0.5 * math.pi)):
                    cx.op("dve", ["posf"], ["rt_a", dkey], lambda h, off=off, dst=dst: h.tensor_scalar(out=dst[:, 0:TT], in0=posf[:, 0:TT], scalar1=off, scalar2=None, op0=ALU.add))
                    cx.op("dve", ["rt_a"], ["rt_u"], lambda h, dst=dst: h.tensor_scalar(out=u[:, 0:TT], in0=dst[:, 0:TT], scalar1=1.0 / TWO_PI, scalar2=0.5, op0=ALU.mult, op1=ALU.add))
                    cx.op("dve", ["rt_u"], ["rt_ki"], lambda h, dst=dst: h.tensor_copy(out=ki[:, 0:TT], in_=u[:, 0:TT]))
                    cx.op("dve", ["rt_ki"], ["rt_kf"], lambda h, dst=dst: h.tensor_copy(out=kf[:, 0:TT], in_=ki[:, 0:TT]))
                    cx.op("dve", ["rt_kf", "rt_a"], ["rt_a"], lambda h, dst=dst: h.scalar_tensor_tensor(out=dst[:, 0:TT], in0=kf[:, 0:TT], scalar=-C1, in1=dst[:, 0:TT], op0=ALU.mult, op1=ALU.add))
                    cx.op("dve", ["rt_kf", "rt_a"], ["rt_a"], lambda h, dst=dst: h.scalar_tensor_tensor(out=dst[:, 0:TT], in0=kf[:, 0:TT], scalar=-C2, in1=dst[:, 0:TT], op0=ALU.mult, op1=ALU.add))
                    cx.op("dve", ["rt_a"], ["rt_u"], lambda h, dst=dst: h.tensor_scalar(out=u[:, 0:TT], in0=dst[:, 0:TT], scalar1=-math.pi, scalar2=TWO_PI, op0=ALU.is_lt, op1=ALU.mult))
                    cx.op("dve", ["rt_a", "rt_u"], ["rt_a"], lambda h, dst=dst: h.tensor_tensor(out=dst[:, 0:TT], in0=dst[:, 0:TT], in1=u[:, 0:TT], op=ALU.add))
                    cx.op("dve", ["rt_a"], ["rt_u"], lambda h, dst=dst: h.tensor_scalar(out=u[:, 0:TT], in0=dst[:, 0:TT], scalar1=math.pi, scalar2=-TWO_PI, op0=ALU.is_gt, op1=ALU.mult))
                    cx.op("dve", ["rt_a", "rt_u"], ["rt_a"], lambda h, dst=dst: h.tensor_tensor(out=dst[:, 0:TT], in0=dst[:, 0:TT], in1=u[:, 0:TT], op=ALU.add))
                    cx.op("dve", ["rt_a"], ["rt_a"], lambda h, dst=dst: h.tensor_scalar(out=dst[:, 0:TT], in0=dst[:, 0:TT], scalar1=-math.pi, scalar2=math.pi, op0=ALU.max, op1=ALU.min))
                    cx.op("act", ["rt_a"], [dkey], lambda h, dst=dst: h.activation(out=dst[:, 0:TT], in_=dst[:, 0:TT], func=AF.Sin))

            qT_d = dint("qT_d", [4, 128, S], BF16)
            kT_d = dint("kT_d", [4, 128, S], BF16)
            v_d = dint("v_d", [S, 2, 257], BF16)

            def phase_a_odd(L0):
                j = JI[L0]
                lam_init = 0.8 - 0.6 * math.exp(-0.3 * L0)
                TT = min(512, TPC)
                NQS = TT // 128
                with ExitStack() as bs:
                    def sb(name, shape, dt):
                        return bs.enter_context(nc.sbuf_tensor("O%d_" % L0 + name, list(shape), dt))
                    owb = sb("owb", [128, 16, 1536], BF16)
                    xT = sb("axT", [128, 16, TT], BF16)
                    cstf = sb("acst", [128, 8], F32)
                    rotm = sb("arot", [128, 128], F32)
                    cosF = sb("cosF", [128, TT], F32)
                    sinF = sb("sinF", [128, TT], F32)
                    posi = sb("posi", [128, TT], I32)
                    posf = sb("posf", [128, TT], F32)
                    u = sb("rtu", [128, TT], F32)
                    ki = sb("rtki", [128, TT], I32)
                    kf = sb("rtkf", [128, TT], F32)
                    tsb = sb("tsb", [128, TT], F32)
                    ta = sb("ta", [128, TT], F32)
                    tb_ = sb("tb", [128, TT], F32)
                    qkb = sb("qkb", [128, 2, TT], BF16)
                    vaug = sb("vaug", [128, 2, 2, 257], BF16)
                    cx.dma("sp", "owb", owb[:], Wb["owin"][j].rearrange("(c p) n -> p c n", p=128), ["c_owin"], ["owb"])
                    cx.dma("sp", "acst", cstf[:], I["cstv"][:, :], [], ["acst"])
                    cx.dma("sp", "arot", rotm[:], I["cst"][:, 1, :], [], ["arot"])
                    cx.op("pool", [], ["vaug"], lambda h: h.memset(vaug[:], 1.0))
                    for tg0 in range(0, S, TT):
                        rp, t0 = tg0 // TPC, tg0 % TPC
                        load_xT(cx, "axT", xT, rp, t0, TT, "axT")
                        rotary_tables(sb, tg0, TT, cstf[:, 0:1], cosF, sinF, posi, posf, u, ki, kf)
                        for ch in range(8):
                            bank = ch % 2
                            for c in range(16):
                                cx.op("pe", ["owb", "axT"], [("ps", bank)], lambda h, c=c, ch=ch, bank=bank: h.matmul(
                                    ps[:, bank, 0:TT], lhsT=owb[:, c, ch * 128:(ch + 1) * 128], rhs=xT[:, c, :], start=(c == 0), stop=(c == 15)))
                            cx.op("act", [("ps", bank)], ["tsb"], lambda h, bank=bank: h.activation(out=tsb[:], in_=ps[:, bank, 0:TT], func=AF.Copy))
                            cx.op("pe", ["arot", "tsb"], [("ps", 2)], lambda h: h.matmul(ps[:, 2, 0:TT], lhsT=rotm[:], rhs=tsb[:], start=True, stop=True))
                            cx.op("dve", ["tsb", "cosF"], ["ta"], lambda h: h.tensor_tensor(out=ta[:], in0=tsb[:], in1=cosF[:], op=ALU.mult))
                            cx.op("dve", [("ps", 2), "sinF"], ["tb"], lambda h: h.tensor_tensor(out=tb_[:], in0=ps[:, 2, 0:TT], in1=sinF[:], op=ALU.mult))
                            kb2 = ch % 2
                            cx.op("pool", ["ta", "tb"], [("qkb", kb2)], lambda h, kb2=kb2: h.tensor_tensor(out=qkb[:, kb2, :], in0=ta[:], in1=tb_[:], op=ALU.add))
                            dst = (qT_d if ch < 4 else kT_d)[ch % 4, :, tg0:tg0 + TT]
                            cx.dma("sp", "qk_d", dst, qkb[:, kb2, :], [("qkb", kb2)], ["qk_d"])
                        for qs in range(NQS):
                            bank = 3 + qs % 2
                            vb = qs % 2
                            for c in range(16):
                                cx.op("pe", ["owb", "axT"], [("ps", bank)], lambda h, c=c, qs=qs, bank=bank: h.matmul(
                                    ps[:, bank, :], lhsT=xT[:, c, qs * 128:(qs + 1) * 128], rhs=owb[:, c, 1024:1536], start=(c == 0), stop=(c == 15)))
                            cx.op("act", [("ps", bank)], [("vaug", vb)], lambda h, bank=bank, vb=vb: h.activation(
                                out=vaug[:, vb, :, 0:256], in_=ps[:, bank, :].rearrange("p (a e) -> p a e", a=2), func=AF.Copy))
                            r0 = tg0 + qs * 128
                            cx.dma("sp", "v_d", v_d[r0:r0 + 128], vaug[:, vb], [("vaug", vb)], ["v_d"])
                cx.barrier()
                NB = S // 128
                QT = min(512, S)
                NQC = QT // 128
                sc = 128.0 ** -0.5
                with ExitStack() as bs:
                    def sb(name, shape, dt):
                        return bs.enter_context(nc.sbuf_tensor("P%d_" % L0 + name, list(shape), dt))
                    kTh = sb("kTh", [128, 2, S], BF16)
                    vh = sb("vh", [128, NB, 257], BF16)
                    qTt = sb("qTt", [128, 2, 2, QT], BF16)
                    PT = sb("PT", [128, 2, QT], BF16)
                    mskf = sb("mskf", [128, 4, 512], F32)
                    msk = sb("msk", [128, 4, 512], BF16)
                    a0 = sb("a0", [128, NQC, 256], F32)
                    o1 = sb("o1", [128, 256], F32)
                    dd = sb("dd", [128, 256], F32)
                    sq = sb("sq", [128, 256], F32)
                    rec = sb("rec", [128, 1], F32)
                    ssq = sb("ssq", [128, 1], F32)
                    lamt = sb("lamt", [128, 4, 128], F32)
                    lamp = sb("lamp", [128, 2, 128], F32)
                    lams = sb("lams", [128, 2], F32)
                    nlam = sb("nlam", [128, 1], F32)
                    subg = sb("subg", [128, 256], F32)
                    yb = sb("yb", [128, 256], BF16)
                    yTs = sb("yTs", [128, 2, 128], BF16)
                    cx.dma("sp", "mskf", mskf[:], I["cmask"][:, 0:4, :], [], ["mskf"])
                    cx.op("dve", ["mskf"], ["msk"], lambda h: h.tensor_copy(out=msk[:], in_=mskf[:]))
                    cx.dma("sp", "lamt", lamt[:], I["olam"][j].partition_broadcast(128), [], ["lamt"])
                    cx.dma("sp", "subg", subg[:], I["osub"][j].partition_broadcast(128), [], ["subg"])
                    cx.op("dve", ["lamt"], ["lamp"], lambda h: h.tensor_tensor(out=lamp[:, 0, :], in0=lamt[:, 0, :], in1=lamt[:, 1, :], op=ALU.mult))
                    cx.op("dve", ["lamt", "lamp"], ["lamp"], lambda h: h.tensor_tensor(out=lamp[:, 1, :], in0=lamt[:, 2, :], in1=lamt[:, 3, :], op=ALU.mult))
                    cx.op("dve", ["lamp"], ["lams"], lambda h: h.tensor_reduce(out=lams[:], in_=lamp[:], axis=AX.X, op=ALU.add))
                    cx.op("act", ["lams"], ["lams"], lambda h: h.activation(out=lams[:], in_=lams[:], func=AF.Exp))
                    cx.op("dve", ["lams"], ["nlam"], lambda h: h.scalar_tensor_tensor(out=nlam[:], in0=lams[:, 1:2], scalar=-lam_init, in1=lams[:, 0:1], op0=ALU.add, op1=ALU.subtract))
                    cx.op("dve", ["subg"], ["subg"], lambda h: h.tensor_scalar(out=subg[:], in0=subg[:], scalar1=(1.0 - lam_init), scalar2=None, op0=ALU.mult))
                    pti = 0
                    for hh in range(2):
                        cx.dma("sp", "kTh", kTh[:], kT_d[2 * hh:2 * hh + 2].rearrange("c p s -> p c s"), ["qk_d"], ["kTh"])
                        cx.dma("sp", "vh", vh[:], v_d[:, hh, :].rearrange("(b p) e -> p b e", p=128), ["v_d"], ["vh"])
                        for qi, q0 in enumerate(range(0, S, QT)):
                            qb = qi % 2
                            cx.dma("sp", "qTt%d" % qb, qTt[:, qb], qT_d[2 * hh:2 * hh + 2, :, q0:q0 + QT].rearrange("c p s -> p c s"), ["qk_d"], [("qTt", qb)])
                            nkb = (q0 + QT) // 128
                            for c in range(2):
                                for kb in range(nkb):
                                    sbk = 4 + pti % 2
                                    pb = pti % 2
                                    pti += 1
                                    cx.op("pe", ["kTh", ("qTt", qb)], [("ps", sbk)], lambda h, c=c, kb=kb, sbk=sbk, qb=qb: h.matmul(
                                        ps[:, sbk, 0:QT], lhsT=kTh[:, c, kb * 128:(kb + 1) * 128], rhs=qTt[:, qb, c, :], start=True, stop=True))
                                    cx.op("act", [("ps", sbk)], [("PT", pb)], lambda h, sbk=sbk, pb=pb: h.activation(out=PT[:, pb, :], in_=ps[:, sbk, 0:QT], func=AF.Exp, scale=sc))
                                    jd = kb - q0 // 128
                                    if jd >= 0:
                                        cx.op("pool", [("PT", pb), "msk"], [("PT", pb)], lambda h, pb=pb, jd=jd: h.tensor_tensor(out=PT[:, pb, :], in0=PT[:, pb, :], in1=msk[:, jd, 0:QT], op=ALU.mult))
                                    for qc in range(NQC):
                                        if jd > qc:
                                            continue
                                        lastkb = q0 // 128 + qc
                                        cx.op("pe", [("PT", pb), "vh"], [("ps", qc)], lambda h, qc=qc, pb=pb, kb=kb, lastkb=lastkb: h.matmul(
                                            ps[:, qc, 0:257], lhsT=PT[:, pb, qc * 128:(qc + 1) * 128], rhs=vh[:, kb, :], start=(kb == 0), stop=(kb == lastkb)))
                                for qc in range(NQC):
                                    cx.op("dve", [("ps", qc)], ["rec"], lambda h, qc=qc: h.reciprocal(out=rec[:], in_=ps[:, qc, 256:257]))
                                    if c == 0:
                                        cx.op("dve", [("ps", qc), "rec"], [("a0", qc)], lambda h, qc=qc: h.tensor_scalar(out=a0[:, qc, :], in0=ps[:, qc, 0:256], scalar1=rec[:, 0:1], scalar2=None, op0=ALU.mult))
                                    else:
                                        cx.op("dve", [("ps", qc), "rec"], ["o1"], lambda h, qc=qc: h.tensor_scalar(out=o1[:], in0=ps[:, qc, 0:256], scalar1=rec[:, 0:1], scalar2=None, op0=ALU.mult))
                                        cx.op("dve", ["o1", "nlam", ("a0", qc)], ["dd"], lambda h, qc=qc: h.scalar_tensor_tensor(out=dd[:], in0=o1[:], scalar=nlam[:, 0:1], in1=a0[:, qc, :], op0=ALU.mult, op1=ALU.add))
                                        cx.op("act", ["dd"], ["sq", "ssq"], lambda h: h.activation(out=sq[:], in_=dd[:], func=AF.Square, accum_out=ssq[:]))
                                        cx.op("act", ["ssq"], ["ssq"], lambda h: h.activation(out=ssq[:], in_=ssq[:], func=AF.Sqrt, bias=1e-5, scale=1.0 / 256.0))
                                        cx.op("dve", ["ssq"], ["ssq"], lambda h: h.reciprocal(out=ssq[:], in_=ssq[:]))
                                        cx.op("dve", ["dd", "ssq", "subg"], ["yb"], lambda h: h.scalar_tensor_tensor(out=yb[:], in0=dd[:], scalar=ssq[:, 0:1], in1=subg[:], op0=ALU.mult, op1=ALU.mult))
                                        for ec in range(2):
                                            cx.op("pe", ["yb", "ident_b"], [("psb", 0)], lambda h, ec=ec: h.transpose(out=psb[:, 0, ec * 128:(ec + 1) * 128], in_=yb[:, ec * 128:(ec + 1) * 128], identity=ident_b[:]))
                                        cx.op("dve", [("psb", 0)], ["yTs"], lambda h: h.tensor_copy(out=yTs[:], in_=psb[:, 0, 0:256].rearrange("p (c t) -> p c t", t=128)))
                                        qg = q0 + qc * 128
                                        jr, tt = qg // TPC, qg % TPC
                                        cx.dma("sp", "yT_own", ydst(jr, tt, hh * 256, 256).rearrange("(c p) t -> p c t", p=128), yTs[:], ["yTs"], ["yT_own"])
                cx.barrier()


            vtok_d = dint("vtok_d", [2, S, 128], F32)

            def phase_a_even(L0):
                j = JI[L0]
                TT = min(256, TPC)
                NQS = TT // 128
                TS = 8
                EH = math.exp(-0.5)
                with ExitStack() as bs:
                    def sb(name, shape, dt):
                        return bs.enter_context(nc.sbuf_tensor("E%d_" % L0 + name, list(shape), dt))
                    ewb = sb("ewb", [128, 16, 1536], BF16)
                    ewvb = sb("ewvb", [128, 16, 64], BF16)
                    xT = sb("exT", [128, 16, TT], BF16)
                    cstf = sb("ecst", [128, 8], F32)
                    rotm = sb("erot", [128, 128], F32)
                    blk = sb("eblk", [128, 128], F32)
                    e01 = sb("e01", [64, 2, 128], F32)
                    mu = sb("emu", [128, 9], F32)
                    prm = sb("eprm", [128, 14], F32)
                    omk = sb("omk", [128, 2], F32)
                    waup = sb("waup", [128, 256], F32)
                    gup = sb("gup", [128, 2, 256], F32)
                    esk = sb("esk", [128, 4], F32)
                    hbuf = sb("hbuf", [128, 9, TT + 1], F32)
                    hp = sb("hp", [128, 9, TT], F32)
                    cosF = sb("ecosF", [128, TT], F32)
                    sinF = sb("esinF", [128, TT], F32)
                    posi = sb("eposi", [128, TT], I32)
                    posf = sb("eposf", [128, TT], F32)
                    u = sb("ertu", [128, TT], F32)
                    ki = sb("ertki", [128, TT], I32)
                    kf = sb("ertkf", [128, TT], F32)
                    tl = sb("tl", [128, TT], F32)
                    sg = sb("sg", [128, 2, TT], F32)
                    t1 = sb("t1", [128, TT], F32)
                    t2 = sb("t2", [128, TT], F32)
                    Wd = sb("Wd", [128, 2, TT], F32)
                    Ag = sb("Ag", [128, 2, TT], F32)
                    Av = sb("Av", [128, 2, TT], F32)
                    Bv = sb("Bv", [128, 2, TT], F32)
                    K2 = sb("K2", [128, 2, TT], F32)
                    Gt = sb("Gt", [128, 2, TT], F32)
                    Bon = sb("Bon", [128, 2, TT], F32)
                    LA = sb("LA", [128, 2, TS, 2, 128], F32)
                    RRt = sb("RRt", [128, 2, TS, 2, 2], F32)
                    VB = sb("VB", [128, 2, TS, 128], F32)
                    KV = sb("KV", [128, 2, TS, 128], F32)
                    STs = sb("STs", [128, 2, 128], F32)
                    S1 = sb("S1", [128, 128], F32)
                    S2 = sb("S2", [128, 128], F32)
                    yA = sb("yA", [64, TT, 4], F32)
                    vtk = sb("vtk", [128, 2, 128], F32)
                    Yf = sb("Yf", [128, TT], F32)
                    yob = sb("yob", [128, TT], BF16)
                    qrot = sb("qrot", [128, 2, TT], BF16)
                    krot = sb("krot", [128, TT + 128], BF16)
                    vau = sb("vau", [128, NQS + 1, 65], BF16)
                    PTs = sb("PTs", [128, 4, 2, 128], BF16)
                    mskf = sb("emskf", [128, 256], F32)
                    msk = sb("emsk", [128, 2, 128], BF16)
                    den = sb("den", [128, 4], F32)
                    ybt = sb("ybt", [128, 4, 64], BF16)
                    ybT = sb("ybT", [128, 2, 128], BF16)
                    cx.dma("sp", "ewb", ewb[:], Wb["ewin"][j].rearrange("(c p) n -> p c n", p=128), ["c_ewin"], ["ewb"])
                    cx.dma("sp", "ewvb", ewvb[:], Wb["ewv"][j].rearrange("(c p) n -> p c n", p=128), ["c_ewv"], ["ewvb"])
                    cx.dma("sp", "ecst", cstf[:], I["cstv"][:, :], [], ["ecst"])
                    cx.dma("sp", "erot", rotm[:], I["cst"][:, 2, :], [], ["erot"])
                    cx.dma("sp", "eblk", blk[:], I["cst"][:, 3, :], [], ["eblk"])
                    cx.dma("sp", "e01", e01[:], I["cst"][0:64, 4:6, :], [], ["e01"])
                    cx.dma("sp", "emu", mu[:], I["emu"][j], [], ["emu"])
                    cx.dma("sp", "eprm", prm[:], I["eprm"][j], [], ["eprm"])
                    cx.dma("sp", "waup", waup[:], I["ewaup"][j], [], ["waup"])
                    cx.dma("sp", "gup", gup[:], I["egup"][j].rearrange("c p n -> p c n"), [], ["gup"])
                    cx.dma("sp", "esk", esk[:], I["esink"][j].partition_broadcast(128), [], ["esk"])
                    cx.dma("sp", "emskf", mskf[:], I["cmask"][:, 4, 0:256], [], ["emskf"])
                    cx.op("dve", ["emskf"], ["emsk"], lambda h: h.tensor_copy(out=msk[:], in_=mskf[:].rearrange("p (a q) -> p a q", a=2)))
                    cx.op("act", ["esk"], ["esk"], lambda h: h.activation(out=esk[:], in_=esk[:], func=AF.Exp))
                    cx.op("dve", ["eprm"], ["omk"], lambda h: h.tensor_scalar(out=omk[:], in0=prm[:, 6:8], scalar1=-1.0, scalar2=1.0, op0=ALU.mult, op1=ALU.add))
                    cx.op("pool", [], [("hbuf", ch) for ch in range(9)], lambda h: h.memset(hbuf[:], 0.0))
                    cx.op("pool", [], ["ST0"], lambda h: h.memset(STs[:], 0.0))
                    cx.op("pool", [], ["vau"], lambda h: h.memset(vau[:], 1.0))
                    cx.op("pool", [], ["krot"], lambda h: h.memset(krot[:], 0.0))
                    W0, A0, KK_, KA, RK, GW, GB = 0, 2, 4, 6, 8, 10, 12
                    stp = 0
                    for tg0 in range(0, S, TT):
                        rp, t0 = tg0 // TPC, tg0 % TPC
                        load_xT(cx, "exT", xT, rp, t0, TT, "exT")
                        rotary_tables(sb, tg0, TT, cstf[:, 1:2], cosF, sinF, posi, posf, u, ki, kf)
                        for ch in range(9):
                            bank = ch % 2
                            for c in range(16):
                                cx.op("pe", ["ewb", "exT"], [("ps", bank)], lambda h, c=c, ch=ch, bank=bank: h.matmul(
                                    ps[:, bank, 0:TT], lhsT=ewb[:, c, ch * 128:(ch + 1) * 128], rhs=xT[:, c, :], start=(c == 0), stop=(c == 15)))
                            cx.op("act", [("ps", bank)], [("hbuf", ch)], lambda h, ch=ch, bank=bank: h.activation(out=hbuf[:, ch, 1:TT + 1], in_=ps[:, bank, 0:TT], func=AF.Copy))
                            cx.op("dve", [("hbuf", ch)], ["t1"], lambda h, ch=ch: h.tensor_tensor(out=t1[:], in0=hbuf[:, ch, 0:TT], in1=hbuf[:, ch, 1:TT + 1], op=ALU.subtract))
                            cx.op("dve", ["t1", ("hbuf", ch), "emu"], [("hp", ch)], lambda h, ch=ch: h.scalar_tensor_tensor(out=hp[:, ch, :], in0=t1[:], scalar=mu[:, ch:ch + 1], in1=hbuf[:, ch, 1:TT + 1], op0=ALU.mult, op1=ALU.add))
                            cx.op("pool", [("hbuf", ch), "t1"], [("hbuf", ch)], lambda h, ch=ch: h.tensor_copy(out=hbuf[:, ch, 0:1], in_=hbuf[:, ch, TT:TT + 1]))
                        cx.op("act", [("hp", 6)], ["tl"], lambda h: h.activation(out=tl[0:64, :], in_=hp[0:64, 6, :], func=AF.Tanh))
                        cx.op("pool", [("hp", 6), "tl"], ["tl"], lambda h: h.tensor_copy(out=tl[64:128, :], in_=hp[64:128, 6, :]))
                        cx.op("act", [("hp", 7), ("hp", 8)], ["sg"], lambda h: h.activation(out=sg[:], in_=hp[:, 7:9, :], func=AF.Sigmoid))
                        for g in range(2):
                            gs = slice(g * 128, (g + 1) * 128)
                            cx.op("pe", ["waup", "tl"], [("ps", 0)], lambda h, gs=gs: h.matmul(ps[:, 0, 0:TT], lhsT=waup[0:64, gs], rhs=tl[0:64, :], start=True, stop=True))
                            cx.op("act", [("ps", 0), "eprm"], ["t1"], lambda h, g=g: h.activation(out=t1[:], in_=ps[:, 0, 0:TT], func=AF.Sigmoid, bias=prm[:, W0 + g:W0 + g + 1], scale=1.0))
                            cx.op("act", ["t1"], [("Wd", g)], lambda h, g=g: h.activation(out=Wd[:, g, :], in_=t1[:], func=AF.Exp, scale=-EH))
                            cx.op("pe", ["waup", "tl"], [("ps", 1)], lambda h, gs=gs: h.matmul(ps[:, 1, 0:TT], lhsT=waup[64:128, gs], rhs=tl[64:128, :], start=True, stop=True))
                            cx.op("act", [("ps", 1), "eprm"], [("Ag", g)], lambda h, g=g: h.activation(out=Ag[:, g, :], in_=ps[:, 1, 0:TT], func=AF.Sigmoid, bias=prm[:, A0 + g:A0 + g + 1], scale=1.0))
                            for c in range(2):
                                cx.op("pe", ["gup", "sg"], [("ps", 2)], lambda h, c=c, gs=gs: h.matmul(ps[:, 2, 0:TT], lhsT=gup[:, c, gs], rhs=sg[:, c, :], start=(c == 0), stop=(c == 1)))
                            cx.op("act", [("ps", 2)], [("Gt", g)], lambda h, g=g: h.activation(out=Gt[:, g, :], in_=ps[:, 2, 0:TT], func=AF.Copy))
                            cx.op("dve", [("hp", 2 + g), "eprm"], ["t1"], lambda h, g=g: h.tensor_scalar(out=t1[:], in0=hp[:, 2 + g, :], scalar1=prm[:, KK_ + g:KK_ + g + 1], scalar2=None, op0=ALU.mult))
                            cx.op("dve", ["t1"], ["t2"], lambda h: h.tensor_tensor(out=t2[:], in0=t1[:], in1=t1[:], op=ALU.mult))
                            cx.op("pe", ["eblk", "t2"], [("ps", 3)], lambda h: h.matmul(ps[:, 3, 0:TT], lhsT=blk[:], rhs=t2[:], start=True, stop=True))
                            cx.op("act", [("ps", 3)], ["t2"], lambda h: h.activation(out=t2[:], in_=ps[:, 3, 0:TT], func=AF.Sqrt))
                            cx.op("dve", ["t2"], ["t2"], lambda h: h.tensor_scalar(out=t2[:], in0=t2[:], scalar1=1e-12, scalar2=None, op0=ALU.max))
                            cx.op("dve", ["t2"], ["t2"], lambda h: h.reciprocal(out=t2[:], in_=t2[:]))
                            cx.op("dve", ["t1", "t2"], ["t1"], lambda h: h.tensor_tensor(out=t1[:], in0=t1[:], in1=t2[:], op=ALU.mult))
                            cx.op("dve", ["t1", ("Ag", g)], [("Bv", g)], lambda h, g=g: h.tensor_tensor(out=Bv[:, g, :], in0=t1[:], in1=Ag[:, g, :], op=ALU.mult))
                            cx.op("dve", ["t1"], [("Av", g)], lambda h, g=g: h.tensor_scalar(out=Av[:, g, :], in0=t1[:], scalar1=-1.0, scalar2=None, op0=ALU.mult))
                            cx.op("dve", [("Ag", g), "eprm", "omk"], ["t2"], lambda h, g=g: h.tensor_scalar(out=t2[:], in0=Ag[:, g, :], scalar1=prm[:, KA + g:KA + g + 1], scalar2=omk[:, g:g + 1], op0=ALU.mult, op1=ALU.add))
                            cx.op("dve", ["t2", ("hp", 2 + g)], [("K2", g)], lambda h, g=g: h.tensor_tensor(out=K2[:, g, :], in0=t2[:], in1=hp[:, 2 + g, :], op=ALU.mult))
                            cx.op("dve", [("hp", g), "eprm", ("K2", g)], ["t1"], lambda h, g=g: h.scalar_tensor_tensor(out=t1[:], in0=hp[:, g, :], scalar=prm[:, RK + g:RK + g + 1], in1=K2[:, g, :], op0=ALU.mult, op1=ALU.mult))
                            cx.op("pe", ["eblk", "t1"], [("ps", 3)], lambda h: h.matmul(ps[:, 3, 0:TT], lhsT=blk[:], rhs=t1[:], start=True, stop=True))
                            cx.op("dve", [("ps", 3), ("hp", 4 + g)], [("Bon", g)], lambda h, g=g: h.tensor_tensor(out=Bon[:, g, :], in0=ps[:, 3, 0:TT], in1=hp[:, 4 + g, :], op=ALU.mult))
                        for qs in range(NQS):
                            for g in range(2):
                                cx.op("pe", [("hp", 4 + g), "ident_f"], [("ps", 2)], lambda h, g=g, qs=qs: h.transpose(out=ps[:, 2, g * 128:(g + 1) * 128], in_=hp[:, 4 + g, qs * 128:(qs + 1) * 128], identity=ident_f[:]))
                            cx.op("act", [("ps", 2)], ["vtk"], lambda h: h.activation(out=vtk[:], in_=ps[:, 2, 0:256].rearrange("p (g i) -> p g i", g=2), func=AF.Copy))
                            r0 = tg0 + qs * 128
                            for p_ in range(2):
                                cx.dma("sp", "vtok_d", vtok_d[p_, r0:r0 + 128, :].rearrange("t (g i) -> t g i", g=2), vtk[:, :, p_ * 64:(p_ + 1) * 64], ["vtk"], ["vtok_d"])
                        for ch in (9, 10, 11):
                            for c in range(16):
                                cx.op("pe", ["ewb", "exT"], [("ps", 0)], lambda h, c=c, ch=ch: h.matmul(
                                    ps[:, 0, 0:TT], lhsT=ewb[:, c, ch * 128:(ch + 1) * 128], rhs=xT[:, c, :], start=(c == 0), stop=(c == 15)))
                            cx.op("act", [("ps", 0)], ["t1"], lambda h: h.activation(out=t1[:], in_=ps[:, 0, 0:TT], func=AF.Copy))
                            cx.op("pe", ["erot", "t1"], [("ps", 1)], lambda h: h.matmul(ps[:, 1, 0:TT], lhsT=rotm[:], rhs=t1[:], start=True, stop=True))
                            cx.op("dve", ["t1", "cosF"], ["t1"], lambda h: h.tensor_tensor(out=t1[:], in0=t1[:], in1=cosF[:], op=ALU.mult))
                            cx.op("dve", [("ps", 1), "sinF"], ["t2"], lambda h: h.tensor_tensor(out=t2[:], in0=ps[:, 1, 0:TT], in1=sinF[:], op=ALU.mult))
                            if ch < 11:
                                cx.op("pool", ["t1", "t2"], ["qrot"], lambda h, ch=ch: h.tensor_tensor(out=qrot[:, ch - 9, :], in0=t1[:], in1=t2[:], op=ALU.add))
                            else:
                                cx.op("pool", ["krot"], ["krot"], lambda h: h.tensor_copy(out=krot[:, 0:128], in_=krot[:, TT:TT + 128]))
                                cx.op("pool", ["t1", "t2", "krot"], ["krot"], lambda h: h.tensor_tensor(out=krot[:, 128:TT + 128], in0=t1[:], in1=t2[:], op=ALU.add))
                        cx.op("pool", ["vau"], ["vau"], lambda h: h.tensor_copy(out=vau[:, 0, 0:64], in_=vau[:, NQS, 0:64]))
                        for qs in range(NQS):
                            for c in range(16):
                                cx.op("pe", ["ewvb", "exT"], [("ps", 0)], lambda h, c=c, qs=qs: h.matmul(ps[:, 0, 0:64], lhsT=xT[:, c, qs * 128:(qs + 1) * 128], rhs=ewvb[:, c, :], start=(c == 0), stop=(c == 15)))
                            cx.op("act", [("ps", 0), "vau"], ["vau"], lambda h, qs=qs: h.activation(out=vau[:, qs + 1, 0:64], in_=ps[:, 0, 0:64], func=AF.Copy))
                        for qs in range(NQS):
                            nblk = (tg0 // 128) + qs
                            for hl in range(4):
                                cq, ph = hl // 2, hl % 2
                                pr = slice(ph * 64, (ph + 1) * 64)
                                for w_ in range(2):
                                    cx.op("pe", ["krot", "qrot"], [("ps", w_)], lambda h, hl=hl, cq=cq, pr=pr, w_=w_, qs=qs: h.matmul(
                                        ps[:, w_, hl * 128:(hl + 1) * 128], lhsT=krot[pr, (qs + w_) * 128:(qs + w_ + 1) * 128], rhs=qrot[pr, cq, qs * 128:(qs + 1) * 128], start=True, stop=True))
                            for w_ in range(2):
                                cx.op("act", [("ps", w_)], ["PTs"], lambda h, w_=w_: h.activation(out=PTs[:, :, w_, :], in_=ps[:, w_, :].rearrange("p (a q) -> p a q", a=4), func=AF.Exp, scale=0.125))
                            cx.op("dve", ["PTs", "emsk"], ["PTs"], lambda h: h.tensor_tensor(out=PTs[:], in0=PTs[:], in1=msk[:].unsqueeze(1).to_broadcast([128, 4, 2, 128]), op=ALU.mult))
                            for hl in range(4):
                                ws = (0, 1) if nblk > 0 else (1,)
                                for w_ in ws:
                                    cx.op("pe", ["PTs", "vau"], [("ps", 2)], lambda h, hl=hl, w_=w_, qs=qs, ws=ws: h.matmul(
                                        ps[:, 2, hl * 65:(hl + 1) * 65], lhsT=PTs[:, hl, w_, :], rhs=vau[:, qs + w_, :], start=(w_ == ws[0]), stop=(w_ == 1)))
                            o4 = ps[:, 2, 0:260].rearrange("p (a e) -> p a e", a=4)
                            cx.op("dve", [("ps", 2), "esk"], ["den"], lambda h, o4=o4: h.tensor_tensor(out=den[:].unsqueeze(2), in0=o4[:, :, 64:65], in1=esk[:].unsqueeze(2), op=ALU.add))
                            cx.op("dve", ["den"], ["den"], lambda h: h.reciprocal(out=den[:], in_=den[:]))
                            cx.op("dve", [("ps", 2), "den"], ["ybt"], lambda h, o4=o4: h.tensor_tensor(out=ybt[:], in0=o4[:, :, 0:64], in1=den[:].unsqueeze(2).to_broadcast([128, 4, 64]), op=ALU.mult))
                            for c in range(2):
                                cx.op("pe", ["ybt", "ident_b"], [("psb", 0)], lambda h, c=c: h.transpose(out=psb[:, 0, c * 128:(c + 1) * 128], in_=ybt[:].rearrange("p a e -> p (a e)")[:, c * 128:(c + 1) * 128], identity=ident_b[:]))
                            cx.op("dve", [("psb", 0)], ["ybT"], lambda h: h.tensor_copy(out=ybT[:], in_=psb[:, 0, 0:256].rearrange("p (c t) -> p c t", t=128)))
                            qg = tg0 + qs * 128
                            jr, tt = qg // TPC, qg % TPC
                            cx.dma("sp", "yT_own", ydst(jr, tt, 256, 256).rearrange("(c p) t -> p c t", p=128), ybT[:], ["ybT"], ["yT_own"])
                        for sbk in range(TT // TS):
                            bb = sbk % 2
                            ts0 = sbk * TS
                            tsl = slice(ts0, ts0 + TS)
                            for p_ in range(2):
                                cx.dma("sp", "VB%d" % bb, VB[p_ * 64:(p_ + 1) * 64, bb], vtok_d[p_, tg0 + ts0:tg0 + ts0 + TS, :].partition_broadcast(64), ["vtok_d"], [("VB", bb)])
                            cx.op("pool", [("VB", bb)] + [("K2", g) for g in range(2)], [("KV", bb)], lambda h, bb=bb, tsl=tsl: h.tensor_tensor(
                                out=KV[:, bb].rearrange("p t (g i) -> p t g i", g=2), in0=VB[:, bb].rearrange("p t (g i) -> p t g i", g=2),
                                in1=K2[:, :, tsl].rearrange("p g t -> p t g").unsqueeze(3).to_broadcast([128, TS, 2, 64]), op=ALU.mult))
                            cx.op("pool", ["eblk"] + [("Av", g) for g in range(2)], [("LA", bb)], lambda h, bb=bb, tsl=tsl: h.tensor_tensor(
                                out=LA[:, bb], in0=blk[:].unsqueeze(1).unsqueeze(1).to_broadcast([128, TS, 2, 128]),
                                in1=Av[:, :, tsl].rearrange("p g t -> p t g").unsqueeze(3).to_broadcast([128, TS, 2, 128]), op=ALU.mult))
                            cx.op("pool", ["ecst"] + [("hp", g) for g in range(2)], [("RRt", bb)], lambda h, bb=bb, tsl=tsl: h.tensor_tensor(
                                out=RRt[:, bb], in0=cstf[:, 2:4].unsqueeze(1).unsqueeze(1).to_broadcast([128, TS, 2, 2]),
                                in1=hp[:, 0:2, tsl].rearrange("p g t -> p t g").unsqueeze(3).to_broadcast([128, TS, 2, 2]), op=ALU.mult))
                            for tt_ in range(TS):
                                t = ts0 + tt_
                                cur, nxt = stp % 2, (stp + 1) % 2
                                stp += 1
                                ck, nk = "ST%d" % cur, "ST%d" % nxt
                                for g in range(2):
                                    cx.op("pe", [("LA", bb), ck], [("ps", 3)], lambda h, g=g, bb=bb, tt_=tt_, cur=cur: h.matmul(
                                        ps[:, 3, g * 64:(g + 1) * 64], lhsT=LA[:, bb, tt_, g, :], rhs=STs[:, cur, g * 64:(g + 1) * 64], start=True, stop=True))
                                for g in range(2):
                                    cx.op("act", [ck, ("Wd", g)], ["S1"], lambda h, g=g, t=t, cur=cur: h.activation(
                                        out=S1[:, g * 64:(g + 1) * 64], in_=STs[:, cur, g * 64:(g + 1) * 64], func=AF.Copy, scale=Wd[:, g, t:t + 1]))
                                cx.op("dve", ["S1", ("KV", bb)], ["S2"], lambda h, bb=bb, tt_=tt_: h.tensor_tensor(out=S2[:], in0=S1[:], in1=KV[:, bb, tt_, :], op=ALU.add))
                                for g in range(2):
                                    cx.op("dve", [("ps", 3), ("Bv", g), "S2"], [nk], lambda h, g=g, t=t, nxt=nxt: h.scalar_tensor_tensor(
                                        out=STs[:, nxt, g * 64:(g + 1) * 64], in0=ps[:, 3, g * 64:(g + 1) * 64], scalar=Bv[:, g, t:t + 1], in1=S2[:, g * 64:(g + 1) * 64], op0=ALU.mult, op1=ALU.add))
                                ybank = 4 + (t // 128) % 2
                                tcol = (t % 128) * 4
                                for g in range(2):
                                    cx.op("pe", [nk, ("RRt", bb)], [("ps", ybank)], lambda h, g=g, bb=bb, tt_=tt_, nxt=nxt, ybank=ybank, tcol=tcol: h.matmul(
                                        ps[0:64, ybank, tcol + 2 * g:tcol + 2 * g + 2], lhsT=STs[:, nxt, g * 64:(g + 1) * 64], rhs=RRt[:, bb, tt_, g, :], start=True, stop=True))
                                if t % 128 == 127:
                                    tb0 = t - 127
                                    cx.op("act", [("ps", ybank)], ["yA"], lambda h, ybank=ybank, tb0=tb0: h.activation(
                                        out=yA[:, tb0:tb0 + 128, :], in_=ps[0:64, ybank, :].rearrange("p (t f) -> p t f", f=4), func=AF.Copy))
                        for g in range(2):
                            for p_ in range(2):
                                cx.op("pe", ["e01", "yA"], [("ps", 0)], lambda h, g=g, p_=p_: h.matmul(ps[:, 0, 0:TT], lhsT=e01[:, p_, :], rhs=yA[:, :, 2 * g + p_], start=(p_ == 0), stop=(p_ == 1)))
                            cx.op("act", [("ps", 0)], ["Yf"], lambda h: h.activation(out=Yf[:], in_=ps[:, 0, 0:TT], func=AF.Copy))
                            cx.op("pe", ["eblk", "Yf"], [("ps", 1)], lambda h: h.matmul(ps[:, 1, 0:TT], lhsT=blk[:], rhs=Yf[:], start=True, stop=True))
                            cx.op("dve", [("ps", 1), "Yf"], ["Yf"], lambda h: h.scalar_tensor_tensor(out=Yf[:], in0=ps[:, 1, 0:TT], scalar=-1.0 / 64.0, in1=Yf[:], op0=ALU.mult, op1=ALU.add))
                            cx.op("dve", ["Yf"], ["t1"], lambda h: h.tensor_tensor(out=t1[:], in0=Yf[:], in1=Yf[:], op=ALU.mult))
                            cx.op("pe", ["eblk", "t1"], [("ps", 1)], lambda h: h.matmul(ps[:, 1, 0:TT], lhsT=blk[:], rhs=t1[:], start=True, stop=True))
                            cx.op("act", [("ps", 1)], ["t1"], lambda h: h.activation(out=t1[:], in_=ps[:, 1, 0:TT], func=AF.Sqrt, bias=64e-5, scale=1.0 / 64.0))
                            cx.op("dve", ["t1"], ["t1"], lambda h: h.reciprocal(out=t1[:], in_=t1[:]))
                            cx.op("dve", ["Yf", "t1"], ["Yf"], lambda h: h.tensor_tensor(out=Yf[:], in0=Yf[:], in1=t1[:], op=ALU.mult))
                            cx.op("dve", ["Yf", "eprm"], ["Yf"], lambda h, g=g: h.tensor_scalar(out=Yf[:], in0=Yf[:], scalar1=prm[:, GW + g:GW + g + 1], scalar2=prm[:, GB + g:GB + g + 1], op0=ALU.mult, op1=ALU.add))
                            cx.op("dve", ["Yf", ("Bon", g)], ["Yf"], lambda h, g=g: h.tensor_tensor(out=Yf[:], in0=Yf[:], in1=Bon[:, g, :], op=ALU.add))
                            cx.op("dve", ["Yf", ("Gt", g)], ["yob"], lambda h, g=g: h.tensor_tensor(out=yob[:], in0=Yf[:], in1=Gt[:, g, :], op=ALU.mult))
                            for qs in range(NQS):
                                qg = tg0 + qs * 128
                                jr, tt = qg // TPC, qg % TPC
                                cx.dma("sp", "yT_own", ydst(jr, tt, g * 128, 128), yob[:, qs * 128:(qs + 1) * 128], ["yob"], ["yT_own"])
                cx.barrier()

            def gather_x():
                for k in range(CH_X):
                    cx.collective("AllGather", [xT_own[k * RX:(k + 1) * RX, :]], [xT_all[k * 4 * RX:(k + 1) * 4 * RX, :]], ["xT_own"], ["xT_all"])

            def gather_y():
                for k in range(CH_Y):
                    cx.collective("AllGather", [yT_own[k * RY:(k + 1) * RY, :]], [yT_all[k * 4 * RY:(k + 1) * 4 * RY, :]], ["yT_own"], ["yT_all"])
                sv = yT_all.rearrange("(j kk r m) t -> j kk r (m t)", j=4, kk=KPJ, r=4)
                dv = y_mine.rearrange("(r kk m) t -> kk r (m t)", r=4, kk=KPJ)
                gy_count[0] += 1
                q = "sp" if gy_count[0] <= 2 else "act"
                rk = rank if q == "sp" else rank_act
                for kk in range(KPJ):
                    cx.dma(q, "y_mine", dv[kk], sv[ds(rk, 1), kk, :, :].rearrange("j r n -> (j r) n"), ["yT_all"], ["y_mine"])

            def prologue():
                with ExitStack() as bs:
                    xin = bs.enter_context(nc.sbuf_tensor("xin", [128, 2, 2048], F32))
                    tmpb = bs.enter_context(nc.sbuf_tensor("ptmpb", [128, 2048], BF16))
                    xtb = bs.enter_context(nc.sbuf_tensor("pxtb", [128, 16, 128], BF16))
                    for bi in range(TPC // 128):
                        k = bi % 2
                        rows = slice(bi * 128, (bi + 1) * 128)
                        cx.dma("sp", "xin%d" % k, xin[:, k, :], I["x_own"][rows, :], [], [("xin", k)])
                        cx.dma("sp", "x_res", x_res[rows, :], xin[:, k, :], [("xin", k)], ["x_res"])
                        emit_xT(xin[:, k, :], ("xin", k), bi * 128, tmpb[:], "ptmpb", xtb[:], "pxtb", xT_own, "xT_own")
                cx.barrier()

            convert_weights()
            if mode == "b":
                cx.dma("pool", "yT_own", yT_own[:, :], I["yT_in"][:, :], [], ["yT_own"], max_dma_last_dim=4096)
                prologue()
                gather_y()
                phase_b(layers[0], True)
            else:
                prologue()
                gather_x()
                for li, L0 in enumerate(layers):
                    last = li == len(layers) - 1
                    (phase_a_even if L0 % 2 == 0 else phase_a_odd)(L0)
                    gather_y()
                    phase_b(L0, last)
                    if not last:
                        gather_x()
            cx.barrier()

    return nc


ROPE_THETA = 500000.0


def _consts():
    cst = np.zeros((128, 8, 128), np.float32)
    cst[:, 0, :] = np.eye(128)
    for m in range(16):
        cst[m + 16, 1, m] = -1.0
        cst[m, 1, m + 16] = 1.0
    for hb in (0, 64):
        for m in range(8):
            cst[hb + m + 8, 2, hb + m] = -1.0
            cst[hb + m, 2, hb + m + 8] = 1.0
    cst[0:64, 3, 0:64] = 1.0
    cst[64:128, 3, 64:128] = 1.0
    cst[0:64, 4, 0:64] = np.eye(64)
    cst[0:64, 5, 64:128] = np.eye(64)
    cstv = np.zeros((128, 8), np.float32)
    f16 = (np.float32(ROPE_THETA) ** (-(np.arange(16, dtype=np.float32) / np.float32(16)))).astype(np.float32)
    f8 = (np.float32(ROPE_THETA) ** (-(np.arange(8, dtype=np.float32) / np.float32(8)))).astype(np.float32)
    for d in range(32):
        cstv[d, 0] = f16[d % 16]
    for p in range(128):
        d = p % 64
        if d < 16:
            cstv[p, 1] = f8[d % 8]
    cstv[0:64, 2] = 1.0
    cstv[64:128, 3] = 1.0
    cmask = np.zeros((128, 6, 512), np.float32)
    k = np.arange(128)[:, None]
    q = np.arange(512)[None, :]
    for jd in range(4):
        cmask[:, jd, :] = (q - k >= 128 * jd)
    q1 = np.arange(128)[None, :]
    cmask[:, 4, 0:128] = (k > q1)
    cmask[:, 4, 128:256] = (k <= q1)
    return cst, cstv, cmask


def make_maps(inp, S, layers):
    TPC = S // 4
    ev = [l // 2 for l in layers if l % 2 == 0]
    od = [l // 2 for l in layers if l % 2 == 1]
    ly = list(layers)
    cst, cstv, cmask = _consts()
    f32 = np.float32
    A = lambda a: np.ascontiguousarray(a)
    def lead(a, idx, shape_tail):
        if len(idx) == 0:
            return np.zeros((1,) + tuple(shape_tail), f32)
        return A(np.asarray(a)[idx])
    shared = {
        "ewout": None, "owout": lead(inp["o_w_out"], od, (D, D)),
        "ln1g": A(inp["ln1_g"][ly][:, None, :]), "ln1b": A(inp["ln1_b"][ly][:, None, :]),
        "ln2g": A(inp["ln2_g"][ly][:, None, :]), "ln2b": A(inp["ln2_b"][ly][:, None, :]),
        "w1": A(inp["ffn_w1"][ly]), "w3": A(inp["ffn_w3"][ly]), "w2": A(inp["ffn_w2"][ly]),
        "wpp": A(inp["ple_w_proj"][ly]), "wpg": A(inp["ple_w_gate"][ly]),
        "cst": cst, "cstv": cstv, "cmask": cmask,
        "olam": lead(inp["o_lambda"], od, (4, 128)),
        "osub": lead(inp["o_subln_g"], od, (256,)).reshape(-1, 1, 256),
    }
    perm = np.concatenate([np.concatenate([np.arange(256 * r, 256 * r + 256), 1024 + np.arange(256 * r, 256 * r + 256)]) for r in range(4)])
    shared["ewout"] = A(np.asarray(inp["e_w_out"])[ev][:, perm, :]) if ev else np.zeros((1, D, D), f32)
    maps = []
    for c in range(8):
        b, r = c // 4, c % 4
        m = dict(shared)
        m["x_own"] = A(inp["x"][b, r * TPC:(r + 1) * TPC])
        m["p_own"] = A(inp["p"][ly][:, b, r * TPC:(r + 1) * TPC])
        m["pos"] = A(np.asarray(inp["positions"])[b, :S][None].astype(np.int32))
        if od:
            w = np.asarray(inp["o_w_in"])[od]
            m["owin"] = A(np.concatenate([w[:, :, 512 * r:512 * r + 512], w[:, :, 2048 + 512 * r:2048 + 512 * r + 512],
                                          w[:, :, 4096 + 512 * r:4096 + 512 * r + 512]], axis=2))
        else:
            m["owin"] = np.zeros((1, D, 1536), f32)
        m.update(_even_maps(inp, ev, r))
        maps.append(m)
    return maps


def _even_maps(inp, ev, r):
    f32 = np.float32
    if not ev:
        return dict(ewin=np.zeros((1, D, 1536), f32), ewv=np.zeros((1, D, 64), f32), emu=np.zeros((1, 128, 9), f32),
                    eprm=np.zeros((1, 128, 14), f32), ewaup=np.zeros((1, 128, 256), f32), egup=np.zeros((1, 2, 128, 256), f32),
                    esink=np.zeros((1, 1, 4), f32))
    n = len(ev)
    w = np.asarray(inp["e_w_in"])[ev]
    mu = np.asarray(inp["e_mu"])[ev]
    AW = 1024
    hs = slice(256 * r, 256 * r + 256)
    cols = np.concatenate([np.arange(256 * r, 256 * r + 256), AW + np.arange(256 * r, 256 * r + 256), 2 * AW + np.arange(256 * r, 256 * r + 256),
                           3 * AW + np.arange(0, 128), 3 * AW + 128 + np.arange(0, 160)])
    ewin = np.zeros((n, D, 1536), f32)
    ewin[:, :, 0:1056] = w[:, :, cols]
    A_COLS = 3360
    qoff = A_COLS + 256 * r
    ewin[:, :, 1152:1408] = w[:, :, qoff:qoff + 256]
    kvh = r // 2
    koff = A_COLS + 1024 + 64 * kvh
    ewin[:, :, 1408:1472] = w[:, :, koff:koff + 64]
    ewin[:, :, 1472:1536] = w[:, :, koff:koff + 64]
    voff = A_COLS + 1024 + 128 + 64 * kvh
    ewv = np.ascontiguousarray(w[:, :, voff:voff + 64])
    emu = np.zeros((n, 128, 9), f32)
    mu_c = np.zeros((n, 1152), f32)
    mu_c[:, 0:1056] = mu[:, cols]
    emu[:] = mu_c.reshape(n, 9, 128).transpose(0, 2, 1)
    eprm = np.zeros((n, 128, 14), f32)
    for i, nm in enumerate(["e_w0", "e_a0", "e_k_k", "e_k_a", "e_r_k", "e_gn_w", "e_gn_b"]):
        a = np.asarray(inp[nm])[ev].reshape(n, 1024)[:, hs]
        eprm[:, :, 2 * i:2 * i + 2] = a.reshape(n, 2, 128).transpose(0, 2, 1)
    ewaup = np.zeros((n, 128, 256), f32)
    ewaup[:, 0:64] = np.asarray(inp["e_w_up"])[ev][:, :, hs]
    ewaup[:, 64:128] = np.asarray(inp["e_a_up"])[ev][:, :, hs]
    egup = np.zeros((n, 2, 128, 256), f32)
    gu = np.asarray(inp["e_g_up"])[ev][:, :, hs]
    egup[:, 0] = gu[:, 0:128]
    egup[:, 1, 0:32] = gu[:, 128:160]
    esink = np.ascontiguousarray(np.asarray(inp["e_sinks"])[ev][:, None, 4 * r:4 * r + 4])
    return dict(ewin=ewin, ewv=ewv, emu=emu, eprm=eprm, ewaup=ewaup, egup=egup, esink=esink)


_NC_CACHE = {}


def kernel(**inputs):
    S = inputs["x"].shape[1]
    layers = (0, 1, 2, 3)
    inp = {k: np.asarray(v) for k, v in inputs.items()}
    maps = make_maps(inp, S, layers)
    key = (S, layers)
    if key not in _NC_CACHE:
        _NC_CACHE[key] = build(S, layers)
    res = run_bass_kernel_spmd(_NC_CACHE[key], maps, core_ids=list(range(8)))
    TPC = S // 4
    out = np.zeros((2, S, D), np.float32)
    for c in range(8):
        b, r = c // 4, c % 4
        out[b, r * TPC:(r + 1) * TPC] = res.results[c]["out"]
    return out
```
